# Optimizing a Trainium2 kernel written in Bass

```python
import jax
import jax.numpy as jnp
from jax import lax
import numpy as np

D_MODEL = 2048
BATCH = 4
SEQ = 2048
DEPTH = 4

GRID_W = 64
CTX_LEN = 256
N_BRANCH = 3
MIX_W = 1024
A_HEADS = 8
A_DK = 128
A_DV = MIX_W // A_HEADS
B_HEADS = 8
B_DK = 64
B_DV = MIX_W // B_HEADS
GK_RANK = 16
GATE_LOGIT_NORMALIZER = 16.0
C_HEADS = 8
C_DH = MIX_W // C_HEADS
KR_MAX = 8
KC = 16
D_FF = 4 * D_MODEL
CHUNK = 16
ROPE_THETA = 10000.0
RMS_EPS = 1e-6
LN_EPS = 1e-5
MASK_VALUE = -1e30
DEEPNORM_ALPHA = (2 * DEPTH) ** 0.25
DEEPNORM_BETA = (8 * DEPTH) ** -0.25
IN_SPLITS = (A_HEADS * A_DK, MIX_W, MIX_W, A_HEADS * A_DK, A_HEADS * A_DK,
             B_HEADS * B_DK, B_HEADS * B_DK, MIX_W, MIX_W, 2 * GK_RANK,
             MIX_W, MIX_W, MIX_W,
             N_BRANCH * D_MODEL)
N_IN = int(sum(IN_SPLITS))
SPLIT_POINTS = tuple(int(s) for s in np.cumsum(IN_SPLITS)[:-1])

kernel_name = 'hybrid_dit_hgrn2_gla_natten'


def to_heads(a, n_heads):
    b, t, w = a.shape
    return a.reshape(b, t, n_heads, w // n_heads).transpose(0, 2, 1, 3)


def from_heads(a):
    b, h, t, d = a.shape
    return a.transpose(0, 2, 1, 3).reshape(b, t, h * d)


def layer_norm(x, g, b):
    xf = x.astype(jnp.float32)
    mu = xf.mean(-1, keepdims=True)
    var = jnp.square(xf - mu).mean(-1, keepdims=True)
    return ((xf - mu) * lax.rsqrt(var + LN_EPS) * g.astype(jnp.float32) + b.astype(jnp.float32)).astype(x.dtype)


def rms_norm_swish_gate(o, gain, gate):
    of = o.astype(jnp.float32)
    of = of * lax.rsqrt(jnp.mean(jnp.square(of), -1, keepdims=True) + RMS_EPS) * gain.astype(jnp.float32)
    return (from_heads(of) * jax.nn.silu(gate.astype(jnp.float32))).astype(gate.dtype)


def axial_rope(t, dim):
    half = dim // 2
    freqs = ROPE_THETA ** (-jnp.arange(0, half, 2, dtype=jnp.float32) / half)
    pos = jnp.arange(t)
    ang_r = (pos // GRID_W).astype(jnp.float32)[:, None] * freqs
    ang_c = (pos % GRID_W).astype(jnp.float32)[:, None] * freqs
    ang = jnp.concatenate([ang_r, ang_r, ang_c, ang_c], axis=-1)
    return jnp.cos(ang), jnp.sin(ang)


def rotate_half(u):
    h = u.shape[-1] // 2
    return jnp.concatenate([-u[..., h:], u[..., :h]], axis=-1)


def apply_axial_rope(x, cos, sin):
    x_r, x_c = jnp.split(x, 2, axis=-1)
    return x * cos + jnp.concatenate([rotate_half(x_r), rotate_half(x_c)], axis=-1) * sin


def chunk_gated_scan(q, k, v, log_a, s0):
    bsz, nh, t, _ = q.shape
    dv = v.shape[-1]
    n = t // CHUNK
    blk = lambda a: a.astype(jnp.float32).reshape(bsz, nh, n, CHUNK, a.shape[-1])
    q, k, v, log_a = blk(q), blk(k), blk(v), blk(log_a)
    b = jnp.cumsum(log_a, axis=3)
    lower = jnp.tril(jnp.ones((CHUNK, CHUNK), dtype=bool))[:, :, None]
    rel = b[:, :, :, :, None, :] - b[:, :, :, None, :, :]
    decay = jnp.where(lower, jnp.exp(jnp.where(lower, rel, 0.0)), 0.0)
    scores = jnp.einsum('bhntd,bhntsd,bhnsd->bhnts', q, decay, k)
    o = jnp.einsum('bhnts,bhnsv->bhntv', scores, v)
    b_end = b[:, :, :, -1, :]
    kv = jnp.einsum('bhnsd,bhnsv->bhndv', k * jnp.exp(b_end[:, :, :, None, :] - b), v)

    def step(s, inp):
        a_end, kv_c = inp
        return a_end[..., None] * s + kv_c, s

    s_final, s_prev = lax.scan(step, s0.astype(jnp.float32),
                               (jnp.moveaxis(jnp.exp(b_end), 2, 0), jnp.moveaxis(kv, 2, 0)))
    o = o + jnp.einsum('bhntd,nbhdv->bhntv', q * jnp.exp(b), s_prev)
    return o.reshape(bsz, nh, t, dv), s_final


def context_final_state(k, v, log_a):
    k, v, log_a = (a.astype(jnp.float32) for a in (k, v, log_a))
    b = jnp.cumsum(log_a, axis=2)
    return jnp.einsum('bhtd,bhtv->bhdv', k * jnp.exp(b[:, :, -1:, :] - b), v)


def one_direction(q_l, k_l, v_l, la_l, q_c, k_c, v_c, la_c, with_ctx_out):
    if with_ctx_out:
        bsz, nh, _, dk = k_c.shape
        o_c, s_c = chunk_gated_scan(q_c, k_c, v_c, la_c, jnp.zeros((bsz, nh, dk, v_c.shape[-1]), jnp.float32))
    else:
        o_c, s_c = None, context_final_state(k_c, v_c, la_c)
    o_l, _ = chunk_gated_scan(q_l, k_l, v_l, la_l, s_c)
    return o_l, o_c


def bidirectional_scan(q_l, k_l, v_l, la_l, q_c, k_c, v_c, la_c, with_ctx_out):
    rev = lambda a: jnp.flip(a, axis=2)
    f_l, f_c = one_direction(q_l, k_l[0], v_l, la_l[0], q_c, k_c[0], v_c, la_c[0], with_ctx_out)
    r_l, r_c = one_direction(rev(q_l), rev(k_l[1]), rev(v_l), rev(la_l[1]),
                             rev(q_c), rev(k_c[1]), rev(v_c), rev(la_c[1]), with_ctx_out)
    o_l = f_l + rev(r_l)
    o_c = f_c + rev(r_c) if with_ctx_out else None
    return o_l, o_c


def hgrn2_forget(f_raw, lb):
    f = f_raw.astype(jnp.float32)
    k = (1.0 - lb) * jax.nn.sigmoid(-f)
    log_a = jnp.log(lb + (1.0 - lb) * jax.nn.sigmoid(f))
    return to_heads(k, A_HEADS), to_heads(log_a, A_HEADS)


def hgrn2_branch(p_l, p_c, lb, norm_g, with_ctx_out):
    def prep(p):
        q, i, g, f_fwd, f_bwd = p
        q = to_heads(jax.nn.silu(q.astype(jnp.float32)), A_HEADS) * (A_DK ** -0.5)
        v = to_heads(i.astype(jnp.float32), A_HEADS)
        k_f, la_f = hgrn2_forget(f_fwd, lb[0])
        k_b, la_b = hgrn2_forget(f_bwd, lb[1])
        return q, (k_f, k_b), v, (la_f, la_b), g

    q_l, k_l, v_l, la_l, g_l = prep(p_l)
    q_c, k_c, v_c, la_c, g_c = prep(p_c)
    o_l, o_c = bidirectional_scan(q_l, k_l, v_l, la_l, q_c, k_c, v_c, la_c, with_ctx_out)
    y_l = rms_norm_swish_gate(o_l, norm_g, g_l)
    y_c = rms_norm_swish_gate(o_c, norm_g, g_c) if with_ctx_out else None
    return y_l, y_c


def gla_branch(p_l, p_c, w_gk2, b_gk2, norm_g, cos, sin, with_ctx_out):
    def decays(gk_low):
        lows = jnp.split(gk_low, 2, axis=-1)
        return tuple(to_heads(jax.nn.log_sigmoid((lo @ w_gk2[d] + b_gk2[d]).astype(jnp.float32))
                              / GATE_LOGIT_NORMALIZER, B_HEADS) for d, lo in enumerate(lows))

    q_l, k_l, v_l, g_l, gk_l = p_l
    q_c, k_c, v_c, g_c, gk_c = p_c
    scale = B_DK ** -0.5
    ql = apply_axial_rope(to_heads(q_l.astype(jnp.float32), B_HEADS), cos, sin) * scale
    kl = apply_axial_rope(to_heads(k_l.astype(jnp.float32), B_HEADS), cos, sin)
    vl = to_heads(v_l, B_HEADS)
    qc = to_heads(q_c.astype(jnp.float32), B_HEADS) * scale
    kc = to_heads(k_c, B_HEADS)
    vc = to_heads(v_c, B_HEADS)
    o_l, o_c = bidirectional_scan(ql, (kl, kl), vl, decays(gk_l), qc, (kc, kc), vc, decays(gk_c), with_ctx_out)
    y_l = rms_norm_swish_gate(o_l, norm_g, g_l)
    y_c = rms_norm_swish_gate(o_c, norm_g, g_c) if with_ctx_out else None
    return y_l, y_c


def neighbourhood_branch(p_l, p_c, rpb, with_ctx_out):
    q, k, v = (to_heads(a, C_HEADS) for a in p_l)
    qc, kc, vc = (to_heads(a, C_HEADS) for a in p_c)
    bsz, nh, t, dh = q.shape
    rows = t // GRID_W
    kr = min(KR_MAX, rows)
    scale = dh ** -0.5
    r = jnp.arange(rows)
    col = jnp.arange(GRID_W)
    row_idx = jnp.clip(r - kr // 2, 0, rows - kr)[:, None] + jnp.arange(kr)[None, :]
    col_start = jnp.clip(col - KC // 2, 0, GRID_W - KC)
    col_ok = (col[None, :] >= col_start[:, None]) & (col[None, :] < col_start[:, None] + KC)
    q_grid = q.reshape(bsz, nh, rows, GRID_W, dh)
    k_band = k.reshape(bsz, nh, rows, GRID_W, dh)[:, :, row_idx]
    v_band = v.reshape(bsz, nh, rows, GRID_W, dh)[:, :, row_idx]
    dr = row_idx - r[:, None] + (KR_MAX - 1)
    dc = jnp.clip(col[None, :] - col[:, None] + (KC - 1), 0, 2 * KC - 2)
    bias = rpb[:, dr[:, None, :, None], dc[None, :, None, :]].astype(jnp.float32)
    s_band = jnp.einsum('bhrqd,bhrkjd->bhrqkj', q_grid, k_band).astype(jnp.float32) * scale + bias
    s_band = jnp.where(col_ok[:, None, :], s_band, MASK_VALUE)
    s_ctx = jnp.einsum('bhtd,bhcd->bhtc', q, kc).astype(jnp.float32) * scale
    n_band = kr * GRID_W
    s = jnp.concatenate([s_band.reshape(bsz, nh, rows, GRID_W, n_band),
                         s_ctx.reshape(bsz, nh, rows, GRID_W, -1)], axis=-1)
    p = jax.nn.softmax(s, axis=-1).astype(v.dtype)
    o = (jnp.einsum('bhrqkj,bhrkjd->bhrqd', p[..., :n_band].reshape(bsz, nh, rows, GRID_W, kr, GRID_W), v_band)
         + jnp.einsum('bhrqc,bhcd->bhrqd', p[..., n_band:], vc))
    y_l = from_heads(o.reshape(bsz, nh, t, dh))
    if with_ctx_out:
        pc = jax.nn.softmax(jnp.einsum('bhsd,bhcd->bhsc', qc, kc).astype(jnp.float32) * scale, axis=-1)
        y_c = from_heads(jnp.einsum('bhsc,bhcd->bhsd', pc.astype(vc.dtype), vc))
    else:
        y_c = None
    return y_l, y_c


def merge_branches(ys, gate_logits, w_branch, w_out):
    gates = jax.nn.sigmoid(gate_logits).reshape(*gate_logits.shape[:-1], N_BRANCH, D_MODEL)
    z = jnp.einsum('btnm,nmd->btnd', jnp.stack(ys, axis=-2), w_branch)
    return jnp.sum(gates * z, axis=-2) @ w_out


def hybrid_mixer(h, h_c, w_in, lb, hgrn_norm_g, gla_w_gk2, gla_b_gk2, gla_norm_g, rpb,
                 w_branch, w_out, cos, sin, with_ctx_out):
    p_l = jnp.split(h @ w_in, SPLIT_POINTS, axis=-1)
    p_c = jnp.split(h_c @ w_in, SPLIT_POINTS, axis=-1)
    a_l, a_c = hgrn2_branch(p_l[0:5], p_c[0:5], lb, hgrn_norm_g, with_ctx_out)
    b_l, b_c = gla_branch(p_l[5:10], p_c[5:10], gla_w_gk2, gla_b_gk2, gla_norm_g, cos, sin, with_ctx_out)
    n_l, n_c = neighbourhood_branch(p_l[10:13], p_c[10:13], rpb, with_ctx_out)
    out_l = merge_branches((a_l, b_l, n_l), p_l[13], w_branch, w_out)
    out_c = merge_branches((a_c, b_c, n_c), p_c[13], w_branch, w_out) if with_ctx_out else None
    return out_l, out_c


def sqrelu_mlp(h, w1, w2):
    return jnp.square(jax.nn.relu(h @ w1)) @ w2


def setup_inputs(seed: int = 0) -> dict:
    key = jax.random.key(seed)
    ks = jax.random.split(key, 24)
    nrm = lambda k, shape, s: jax.random.normal(k, shape, jnp.float32) * s
    return {
        'x': nrm(ks[0], (BATCH, SEQ, D_MODEL), 1.0),
        'c': nrm(ks[1], (BATCH, D_MODEL), 1.0),
        'ctx': nrm(ks[2], (BATCH, CTX_LEN, D_MODEL), 1.0),
        'c_ctx': nrm(ks[3], (D_MODEL,), 1.0),
        'w_ada': nrm(ks[4], (DEPTH, D_MODEL, 6 * D_MODEL), 0.5 * D_MODEL ** -0.5),
        'b_ada': nrm(ks[5], (DEPTH, 6 * D_MODEL), 0.02),
        'w_in': nrm(ks[6], (DEPTH, D_MODEL, N_IN), D_MODEL ** -0.5),
        'hgrn_lb_logits': nrm(ks[7], (DEPTH, 2, A_HEADS * A_DK), 0.5),
        'hgrn_norm_g': 1.0 + nrm(ks[8], (DEPTH, A_DV), 0.02),
        'gla_w_gk2': nrm(ks[9], (DEPTH, 2, GK_RANK, B_HEADS * B_DK), GK_RANK ** -0.5),
        'gla_b_gk2': nrm(ks[10], (DEPTH, 2, B_HEADS * B_DK), 0.02),
        'gla_norm_g': 1.0 + nrm(ks[11], (DEPTH, B_DV), 0.02),
        'natten_rpb': nrm(ks[12], (DEPTH, C_HEADS, 2 * KR_MAX - 1, 2 * KC - 1), 0.02),
        'w_branch': nrm(ks[13], (DEPTH, N_BRANCH, MIX_W, D_MODEL), DEEPNORM_BETA * MIX_W ** -0.5),
        'w_out': nrm(ks[14], (DEPTH, D_MODEL, D_MODEL), DEEPNORM_BETA * D_MODEL ** -0.5),
        'ln1_g': 1.0 + nrm(ks[15], (DEPTH, D_MODEL), 0.02),
        'ln1_b': nrm(ks[16], (DEPTH, D_MODEL), 0.02),
        'ln2_g': 1.0 + nrm(ks[17], (DEPTH, D_MODEL), 0.02),
        'ln2_b': nrm(ks[18], (DEPTH, D_MODEL), 0.02),
        'w_mlp1': nrm(ks[19], (DEPTH, D_MODEL, D_FF), D_MODEL ** -0.5),
        'w_mlp2': nrm(ks[20], (DEPTH, D_FF, D_MODEL), DEEPNORM_BETA * D_FF ** -0.5),
    }


def reference(x, c, ctx, c_ctx, w_ada, b_ada, w_in, hgrn_lb_logits, hgrn_norm_g, gla_w_gk2, gla_b_gk2,
              gla_norm_g, natten_rpb, w_branch, w_out, ln1_g, ln1_b, ln2_g, ln2_b, w_mlp1, w_mlp2):
    cos, sin = axial_rope(x.shape[1], B_DK)
    lb_p = jax.nn.softmax(hgrn_lb_logits.astype(jnp.float32), axis=0)
    lb_cum = jnp.cumsum(lb_p, axis=0)
    lower_bounds = jnp.concatenate([jnp.zeros_like(lb_cum[:1]), lb_cum[:-1]], axis=0)
    for l in range(DEPTH):
        with_ctx_out = l < DEPTH - 1
        mod = jax.nn.silu(c) @ w_ada[l] + b_ada[l]
        mod_c = jax.nn.silu(c_ctx) @ w_ada[l] + b_ada[l]
        sh1, sc1, g1, sh2, sc2, g2 = (m[:, None, :] for m in jnp.split(mod, 6, axis=-1))
        sh1c, sc1c, g1c, sh2c, sc2c, g2c = jnp.split(mod_c, 6, axis=-1)
        h = x * (1.0 + sc1) + sh1
        h_c = ctx * (1.0 + sc1c) + sh1c
        mix, mix_c = hybrid_mixer(h, h_c, w_in[l], lower_bounds[l], hgrn_norm_g[l], gla_w_gk2[l], gla_b_gk2[l],
                                  gla_norm_g[l], natten_rpb[l], w_branch[l], w_out[l], cos, sin, with_ctx_out)
        x = layer_norm(DEEPNORM_ALPHA * x + g1 * mix, ln1_g[l], ln1_b[l])
        x = layer_norm(DEEPNORM_ALPHA * x + g2 * sqrelu_mlp(x * (1.0 + sc2) + sh2, w_mlp1[l], w_mlp2[l]),
                       ln2_g[l], ln2_b[l])
        if with_ctx_out:
            ctx = layer_norm(DEEPNORM_ALPHA * ctx + g1c * mix_c, ln1_g[l], ln1_b[l])
            ctx = layer_norm(DEEPNORM_ALPHA * ctx + g2c * sqrelu_mlp(ctx * (1.0 + sc2c) + sh2c, w_mlp1[l], w_mlp2[l]),
                             ln2_g[l], ln2_b[l])
    return x
```

```python
import numpy as np
from contextlib import ExitStack
import concourse.bass as bass
import concourse.mybir as mybir
from concourse.bass_utils import run_bass_kernel_spmd

F32 = mybir.dt.float32
BF16 = mybir.dt.bfloat16
AF = mybir.ActivationFunctionType
ALU = mybir.AluOpType

D = 2048
DEPTH = 4
NT = 1280
NCTX = 256
NLAT = 1024
KC = 16
ALPHA = (2 * DEPTH) ** 0.25
TT = ((0, 512), (512, 1024), (1024, 1280))
CR = ((0, 256, 0), (256, 1280, 1))
OFF_AQ, OFF_AI, OFF_AG, OFF_AF0, OFF_AF1 = 0, 1024, 2048, 3072, 4096
OFF_BQ, OFF_BK, OFF_BV, OFF_BG, OFF_BGK = 5120, 5632, 6144, 7168, 8192
OFF_CQ, OFF_CK, OFF_CV, OFF_GATE = 8224, 9248, 10272, 11296
N_ADA = 96
SAME_SYNC = True
LAYERS = DEPTH
STOP = 99
N_DUMMY3 = 0
N_DUMMY2 = 0
N_DUMMY = 0
NH_A = 8
CC_LIMIT = 10 ** 9
DEBUG = False


class _StopBuild(Exception):
    pass

SP_LB = 0
SP_HNG = SP_LB + 64
SP_GNG = SP_HNG + 4
SP_BGK = SP_GNG + 4
SP_PM = SP_BGK + 32
SP_N = SP_PM + 2
PL_BADA = 0
PL_LN = 96
PL_N = 96 + 64
CF_ONES = 0
CF_MASK = 128
CF_N = CF_MASK + 512


def _unit_cols(W, cols):
    K = W.shape[0]
    sub = W[:, cols]
    return np.ascontiguousarray(sub.reshape(K // 128, 128, len(cols)).transpose(1, 0, 2)).reshape(128, -1)


def _rope_perm():
    perm = np.zeros(64, np.int64)
    sign = np.zeros(64, np.float32)
    for blk in range(2):
        for i in range(32):
            if i < 16:
                perm[blk * 32 + i] = blk * 32 + i + 16
                sign[blk * 32 + i] = -1.0
            else:
                perm[blk * 32 + i] = blk * 32 + i - 16
                sign[blk * 32 + i] = 1.0
    return perm, sign


def _build_common_units(inp, l):
    w_in = inp['w_in'][l]
    units = []
    ar = np.arange(128)
    for j in range(8):
        units.append(_unit_cols(w_in, OFF_AQ + j * 128 + ar))
        units.append(_unit_cols(w_in, OFF_AI + j * 128 + ar))
        units.append(_unit_cols(w_in, OFF_AG + j * 128 + ar))
        units.append(None)
        units.append(None)
    units.append(None)
    perm, _ = _rope_perm()
    for p in range(4):
        cq = OFF_BQ + p * 128 + ar
        ck = OFF_BK + p * 128 + ar
        pp = np.concatenate([perm, 64 + perm])
        units.append(_unit_cols(w_in, cq))
        units.append(_unit_cols(w_in, OFF_BQ + p * 128 + pp))
        units.append(_unit_cols(w_in, ck))
        units.append(_unit_cols(w_in, OFF_BK + p * 128 + pp))
        for hh in range(2):
            units.append(_unit_cols(w_in, OFF_BV + (2 * p + hh) * 128 + ar))
        for hh in range(2):
            units.append(_unit_cols(w_in, OFF_BG + (2 * p + hh) * 128 + ar))
    for j in range(8):
        units.append(_unit_cols(w_in, OFF_CQ + j * 128 + ar))
        units.append(_unit_cols(w_in, OFF_CK + j * 128 + ar))
        units.append(_unit_cols(w_in, OFF_CV + j * 128 + ar))
    wbr = inp['w_branch'][l]
    for op in range(8):
        for n in range(3):
            units.append(_unit_cols(wbr[n], op * 256 + np.arange(256)))
            units.append(_unit_cols(w_in, OFF_GATE + n * D + (2 * op) * 128 + ar))
            units.append(_unit_cols(w_in, OFF_GATE + n * D + (2 * op + 1) * 128 + ar))
    wo = inp['w_out'][l]
    for o in range(16):
        units.append(_unit_cols(wo, o * 128 + ar))
    w1 = inp['w_mlp1'][l]
    w2 = inp['w_mlp2'][l]
    for gi in range(16):
        for j in range(4):
            units.append(_unit_cols(w1, (gi * 4 + j) * 128 + ar))
        for q in range(4):
            blk = w2[gi * 512:(gi + 1) * 512, q * 512:(q + 1) * 512]
            units.append(np.ascontiguousarray(blk.reshape(4, 128, 512).transpose(1, 0, 2)).reshape(128, -1))
    return units


def _build_dir_units(inp, l, side):
    w_in = inp['w_in'][l]
    ar = np.arange(128)
    f0, f1 = (OFF_AF0, OFF_AF1) if side == 0 else (OFF_AF1, OFF_AF0)
    units = []
    for j in range(8):
        units.append(_unit_cols(w_in, f0 + j * 128 + ar))
        units.append(_unit_cols(w_in, f1 + j * 128 + ar))
    W = np.zeros((D, 128), np.float32)
    g0, g1 = (0, 16) if side == 0 else (16, 0)
    W[:, 0:16] = w_in[:, OFF_BGK + g0:OFF_BGK + g0 + 16]
    W[:, 32:48] = w_in[:, OFF_BGK + g1:OFF_BGK + g1 + 16]
    units.append(_unit_cols(W, ar))
    return units


def _global_tok(side, tl):
    return tl if side == 0 else 2047 - tl


def _build_nbias_idx(side):
    i = np.arange(8)[:, None, None]
    qq = np.arange(128)[None, :, None]
    n = np.arange(640)[None, None, :]
    tq = _global_tok(side, 128 * i + qq)
    rq, cq = tq // 64, tq % 64
    lo = np.clip(2 * i - 4, 0, 10)
    e = 64 * lo + n
    own = e < 1024
    m = (e - 1024) // 64
    cc = (e - 1024) % 64
    pl = (15 - m) * 64 + cc
    tk_own = _global_tok(side, np.minimum(e, 1023))
    tk_halo = _global_tok(1 - side, np.clip(pl, 0, 1023))
    tk = np.where(own, tk_own, tk_halo)
    rk, ck = tk // 64, tk % 64
    shp = (8, 128, 640)
    rq = np.broadcast_to(rq, shp)
    cq = np.broadcast_to(cq, shp)
    rk = np.broadcast_to(rk, shp)
    ck = np.broadcast_to(ck, shp)
    start = np.clip(rq - 4, 0, 24)
    cs = np.clip(cq - 8, 0, 48)
    valid = (rk >= start) & (rk < start + 8) & (ck >= cs) & (ck < cs + 16)
    dr = np.clip(rk - rq + 7, 0, 14)
    dc = np.clip(ck - cq + 15, 0, 30)
    return valid, dr, dc


def _rope_tables(side):
    half = 32
    freqs = (10000.0 ** (-np.arange(0, half, 2, dtype=np.float32) / half)).astype(np.float32)
    tl = np.arange(NLAT)
    tg = _global_tok(side, tl)
    ang_r = (tg // 64).astype(np.float32)[:, None] * freqs
    ang_c = (tg % 64).astype(np.float32)[:, None] * freqs
    ang = np.concatenate([ang_r, ang_r, ang_c, ang_c], axis=-1)
    cos = np.ones((NT, 64), np.float32)
    sin = np.zeros((NT, 64), np.float32)
    cos[NCTX:] = np.cos(ang)
    sin[NCTX:] = np.sin(ang)
    _, sign = _rope_perm()
    sinS = sin * sign[None, :]
    cosT = np.concatenate([cos.T, cos.T], axis=0)
    sinT = np.concatenate([sinS.T, sinS.T], axis=0)
    return np.ascontiguousarray(cosT), np.ascontiguousarray(sinT)


def _const_f32():
    cf = np.zeros((128, CF_N), np.float32)
    cf[:, CF_ONES:CF_ONES + 128] = 1.0
    s = np.arange(128)[:, None]
    t = np.arange(128)[None, :]
    for k, (C, fwd) in enumerate(((64, True), (64, False), (128, True), (128, False))):
        same = (s // C) == (t // C)
        m = same & ((s <= t) if fwd else (s >= t))
        cf[:, CF_MASK + k * 128:CF_MASK + (k + 1) * 128] = m.astype(np.float32)
    return cf


def _reset_masks():
    rm = np.ones((128, 2, NT), np.float32)
    rm[:, 0, ::64] = 0.0
    rm[:, 1, ::128] = 0.0
    return rm


def _small_params(inp, side, nl):
    sp = np.zeros((128, SP_N), np.float32)
    for l in range(DEPTH):
        for d in range(2):
            dsrc = d if side == 0 else 1 - d
            sp[:, SP_LB + l * 16 + d * 8:SP_LB + l * 16 + d * 8 + 8] = inp['hgrn_lb_logits'][l, dsrc].reshape(8, 128).T
    for l in range(nl):
        for d in range(2):
            dsrc = d if side == 0 else 1 - d
            sp[:, SP_BGK + l * 8 + d * 4:SP_BGK + l * 8 + d * 4 + 4] = inp['gla_b_gk2'][l, dsrc].reshape(4, 128).T
        sp[:, SP_HNG + l] = inp['hgrn_norm_g'][l]
        sp[:, SP_GNG + l] = inp['gla_norm_g'][l]
    sp[:, SP_PM + 0] = 0.0 if side == 0 else 1.0
    sp[:, SP_PM + 1] = 1.0 if side == 0 else 0.0
    return sp


def _layer_params(inp, nl):
    pl = np.zeros((nl, 128, PL_N), np.float32)
    for l in range(nl):
        pl[l, :, PL_BADA:PL_BADA + 96] = inp['b_ada'][l].reshape(96, 128).T
        for k, nm in enumerate(('ln1_g', 'ln1_b', 'ln2_g', 'ln2_b')):
            pl[l, :, PL_LN + k * 16:PL_LN + (k + 1) * 16] = inp[nm][l].reshape(16, 128).T
    return pl


def _wgk(inp, side, nl):
    w = np.zeros((64, nl, 512), np.float32)
    for l in range(nl):
        d0, d1 = (0, 1) if side == 0 else (1, 0)
        w[0:16, l] = inp['gla_w_gk2'][l, d0]
        w[32:48, l] = inp['gla_w_gk2'][l, d1]
    return w


class Res:
    __slots__ = ('name', 'lw', 'rd')

    def __init__(self, name):
        self.name = name
        self.lw = None
        self.rd = {}


class Rec:
    ENG = ('pe', 'act', 'dve', 'pool', 'sp')
    EP = 12000
    EPD = 120
    R = 8

    def __init__(self):
        self.prog = {e: [] for e in self.ENG}
        self.cnt = {}
        self.seen = {e: {} for e in self.ENG}
        self.dmak = {'pool': 0, 'sp': 0}
        self.cck = 0
        self.out_dmas = []

    def op(self, eng, fn, reads=(), writes=(), kind='c'):
        d = {}
        raw_same = 0
        for r in reads:
            if r.lw is not None:
                d[r.lw[0]] = max(d.get(r.lw[0], 0), r.lw[1])
                if r.lw[0] == eng:
                    raw_same = max(raw_same, r.lw[1])
        for w in writes:
            if w.lw is not None:
                d[w.lw[0]] = max(d.get(w.lw[0], 0), w.lw[1])
            for e2, i2 in w.rd.items():
                d[e2] = max(d.get(e2, 0), i2)
        if kind == 'c':
            ve = eng
        elif kind == 'dma':
            k = self.dmak[eng]
            self.dmak[eng] += 1
            ve = '%sd%d' % (eng, k % self.R)
        else:
            ve = 'cc%d' % (self.cck % 4)
            self.cck += 1
        idx = self.cnt.get(ve, 0) + 1
        self.cnt[ve] = idx
        waits = []
        for e2, i2 in d.items():
            if e2 == eng and kind == 'c':
                if eng == 'pe' or not SAME_SYNC:
                    continue
                i2 = raw_same
                if i2 == 0:
                    continue
            if i2 > self.seen[eng].get(e2, 0):
                waits.append((e2, i2))
                self.seen[eng][e2] = i2
        self.prog[eng].append((waits, fn, ve, idx, kind))
        for r in reads:
            r.rd[ve] = max(r.rd.get(ve, 0), idx)
        for w in writes:
            w.lw = (ve, idx)
            w.rd = {}
        return (ve, idx)

    def barrier(self):
        snap = dict(self.cnt)
        for eng in self.ENG:
            waits = []
            for e2, i2 in snap.items():
                if e2 == eng:
                    continue
                if i2 > self.seen[eng].get(e2, 0):
                    waits.append((e2, i2))
                    self.seen[eng][e2] = i2
            if waits:
                self.prog[eng].append((waits, None, None, 0, 'w'))

    def sem_key(self, ve, idx):
        if ve in self.ENG:
            return '%s_%d' % (ve, (idx - 1) // self.EP), (idx - 1) % self.EP + 1
        if ve.startswith('cc'):
            return ve, idx
        return '%s_%d' % (ve, (idx - 1) // self.EPD), 16 * ((idx - 1) % self.EPD + 1)

    def all_sem_names(self):
        names = []
        for ve, n in self.cnt.items():
            if ve in self.ENG:
                names += ['%s_%d' % (ve, k) for k in range((n - 1) // self.EP + 1)]
            elif ve.startswith('cc'):
                names.append(ve)
            else:
                names += ['%s_%d' % (ve, k) for k in range((n - 1) // self.EPD + 1)]
        return names

    def emit(self, eng, e, sems):
        for waits, fn, ve, idx, kind in self.prog[eng]:
            for e2, i2 in waits:
                nm, val = self.sem_key(e2, i2)
                e.wait_ge(sems[nm], val)
            if fn is None:
                continue
            ins = fn(e)
            nm, val = self.sem_key(ve, idx)
            if kind == 'c':
                ins.then_inc(sems[nm], 1)
            elif kind == 'dma':
                ins.then_inc(sems[nm], 16)
            else:
                ins.then_inc(sems[nm])


def build_program(nl):
    nc = bass.Bass("TRN2", target_bir_lowering=False)
    rec = Rec()
    es = ExitStack()
    NU_C = len(_UNIT_KINDS) if STOP > 50 else 176
    NBH = 8
    n_dir = 17
    xin = nc.dram_tensor("xin", [128, KC, NT], F32, kind="ExternalInput").ap()
    cin = nc.dram_tensor("cin", [128, KC, 2], F32, kind="ExternalInput").ap()
    wada = nc.dram_tensor("wada", [nl, N_ADA, 128, 2048], F32, kind="ExternalInput").ap()
    wsc = nc.dram_tensor("wsc", [nl, NU_C, 128, 2048], F32, kind="ExternalInput").ap()
    wsd = nc.dram_tensor("wsd", [nl, n_dir, 128, 2048], F32, kind="ExternalInput").ap()
    spin = nc.dram_tensor("spin", [128, SP_N], F32, kind="ExternalInput").ap()
    plin = nc.dram_tensor("plin", [nl, 128, PL_N], F32, kind="ExternalInput").ap()
    cfin = nc.dram_tensor("cfin", [128, CF_N], F32, kind="ExternalInput").ap()
    rmin = nc.dram_tensor("rmin", [128, 2, NT], F32, kind="ExternalInput").ap()
    ropein = nc.dram_tensor("ropein", [128, 2, NT], F32, kind="ExternalInput").ap()
    wgkin = nc.dram_tensor("wgkin", [64, nl, 512], F32, kind="ExternalInput").ap()
    nbin = nc.dram_tensor("nbin", [nl, NBH, 8, 128, 640], F32, kind="ExternalInput").ap()
    outT = nc.dram_tensor("outT", [128, KC, NLAT], F32, kind="ExternalOutput").ap()
    if DEBUG:
        dbgy = nc.dram_tensor("dbgy", [128, 24, NT], F32, kind="ExternalOutput").ap()
    xs_t = nc.dram_tensor("xspill", [128, KC * NT], F32)
    xs = xs_t.ap()

    NW = 50432
    AR = es.enter_context(nc.sbuf_tensor("arena", [128, NW], F32))
    SPT = es.enter_context(nc.sbuf_tensor("spt", [128, SP_N], F32))
    PLT = es.enter_context(nc.sbuf_tensor("plt", [128, PL_N], F32))
    CFT = es.enter_context(nc.sbuf_tensor("cft", [128, 128], F32))
    MSK = es.enter_context(nc.sbuf_tensor("msk", [128, 512], BF16))
    IDB = es.enter_context(nc.sbuf_tensor("idb", [128, 128], BF16))
    MODT = es.enter_context(nc.sbuf_tensor("modt", [128, 96, 2], F32))
    SC1P = es.enter_context(nc.sbuf_tensor("sc1p", [128, 32, 2], F32))
    SCT = es.enter_context(nc.sbuf_tensor("sct", [128, KC, 2], F32))
    LBT = es.enter_context(nc.sbuf_tensor("lbt", [128, 3, 64], F32))
    WGK = es.enter_context(nc.sbuf_tensor("wgk", [64, 512], BF16))
    SML = es.enter_context(nc.sbuf_tensor("sml", [128, 48], F32))
    PA = es.enter_context(nc.psum_tensor("pa", [128, 1536], F32))
    PB = es.enter_context(nc.psum_tensor("pb", [128, 1536], F32))
    P6 = es.enter_context(nc.psum_tensor("p6", [128, 1024], BF16))
    P7 = es.enter_context(nc.psum_tensor("p7", [128, 512], F32))
    rPA, rPB, rP6 = Res('pa'), Res('pb'), Res('p6')
    rP7a, rP7b, rP7c = Res('p7a'), Res('p7b'), Res('p7c')
    rSP, rCF, rID, rMOD, rSCT, rLB, rWGK = Res('sp'), Res('cf'), Res('id'), Res('mod'), Res('sct'), Res('lb'), Res('wgk')

    def fv(off, n):
        return AR[:, off:off + n]

    def bv(off, nwords):
        return AR[:, off:off + nwords].bitcast(BF16)

    O_X = 0
    O_H = 20480
    O_U = 30720
    O_WB = 40960
    NWB = 3
    O_T = O_WB + NWB * 1024
    xT = fv(O_X, 20480).rearrange("p (a b) -> p a b", a=KC)
    yT = bv(O_X, 15360).rearrange("p (a b) -> p a b", a=24)
    hT = bv(O_H, 10240).rearrange("p (a b) -> p a b", a=KC)
    uT = bv(O_U, 10240).rearrange("p (a b) -> p a b", a=KC)
    rX, rH, rU, rY = Res('x'), Res('h'), Res('u'), Res('y')
    rYs = [Res('y%d' % i) for i in range(24)]
    rXk = [Res('x%d' % i) for i in range(KC)]

    ones = CFT[:, CF_ONES:CF_ONES + 128]

    def maskT(k):
        return MSK[:, k * 128:(k + 1) * 128]

    def spc(c0, n=1):
        return SPT[:, c0:c0 + n]

    class WStream:
        def __init__(self, l):
            self.l = l
            self.k = 0
            self.issued = 0
            self.ci = 0
            self.di = 0
            self.src = []
            for kind in _UNIT_KINDS_FULL:
                if kind == 'c':
                    self.src.append((wsc, self.ci))
                    self.ci += 1
                else:
                    self.src.append((wsd, self.di))
                    self.di += 1
            self.res = [Res('wb%d' % i) for i in range(NWB)]

        def _issue(self):
            k = self.issued
            if k >= len(self.src):
                return
            slot = k % NWB
            dst = bv(O_WB + slot * 1024, 1024)
            src = self.src[k][0][self.l, self.src[k][1]]
            rec.op('pool', lambda e, dst=dst, src=src: e.dma_start(out=dst, in_=src), writes=[self.res[slot]], kind='dma')
            self.issued += 1

        def prefetch(self, n):
            while self.issued < min(len(self.src), self.k + n):
                self._issue()

        def skip(self, n):
            assert self.issued <= self.k + 0 or True
            self.k += n
            self.issued = max(self.issued, self.k)

        def next(self, held=0):
            self.prefetch((NWB - held) if NH_A == 8 else 1)
            if self.issued <= self.k:
                self._issue()
            slot = self.k % NWB
            self.k += 1
            return bv(O_WB + slot * 1024, 1024), self.res[slot]

    def mm(out, lhsT, rhs, start, stop, reads, writes):
        rec.op('pe', lambda e: e.matmul(out, lhsT=lhsT, rhs=rhs, start=start, stop=stop), reads=reads, writes=writes)

    def act(out, in_, func, reads, writes, bias=None, scale=None, accum=None):
        kw = {}
        if bias is not None:
            kw['bias'] = bias
        if scale is not None:
            kw['scale'] = scale
        if accum is not None:
            kw['accum_out'] = accum
        rec.op('act', lambda e: e.activation(out=out, in_=in_, func=func, **kw), reads=reads, writes=writes)

    def vtt(out, a, b, op, reads, writes, eng='dve'):
        rec.op(eng, lambda e: e.tensor_tensor(out=out, in0=a, in1=b, op=op), reads=reads, writes=writes)

    def vts(out, a, s1, s2, op0, op1, reads, writes, eng='dve'):
        if s2 is None:
            rec.op(eng, lambda e: e.tensor_scalar(out=out, in0=a, scalar1=s1, scalar2=None, op0=op0), reads=reads, writes=writes)
        else:
            rec.op(eng, lambda e: e.tensor_scalar(out=out, in0=a, scalar1=s1, scalar2=s2, op0=op0, op1=op1), reads=reads, writes=writes)

    def vstt(out, in0, scalar, in1, op0, op1, reads, writes, eng='dve'):
        rec.op(eng, lambda e: e.scalar_tensor_tensor(out=out, in0=in0, scalar=scalar, in1=in1, op0=op0, op1=op1), reads=reads, writes=writes)

    def dma(eng, out, in_, reads, writes):
        return rec.op(eng, lambda e: e.dma_start(out=out, in_=in_), reads=reads, writes=writes, kind='dma')

    def proj_fm(w3, wres, a3, ares, ps, pres, nk=KC):
        for (t0, t1) in TT:
            for kc in range(nk):
                mm(ps[:, t0:t1], w3[:, kc, :], a3[:, kc, t0:t1], kc == 0, kc == nk - 1, [wres, ares], [pres])

    def proj_tm(w3, wres, a3, ares, ps, pres, nk=KC):
        for tt in range(10):
            for kc in range(nk):
                mm(ps[:, tt * 128:(tt + 1) * 128], a3[:, kc, tt * 128:(tt + 1) * 128], w3[:, kc, :], kc == 0, kc == nk - 1, [wres, ares], [pres])

    def w3of(wap, nk=KC, nc_=128):
        return wap.rearrange("p (a b) -> p a b", a=nk)

    dma('sp', SPT[:, :], spin[:, :], [], [rSP])
    dma('sp', CFT[:, :], cfin[:, 0:128], [], [rCF])
    dma('pool', MSK[:, :], cfin[:, CF_MASK:CF_MASK + 512], [], [rCF])
    dma('sp', SCT[:, :, :], cin[:, :, :], [], [rSCT])
    for kc in range(KC):
        dma('pool', xT[:, kc, :], xin[:, kc, :], [], [rX])
    rec.op('pool', lambda e: e.memset(IDB[:], 0.0), writes=[rID])
    rec.op('pool', lambda e: e.affine_select(out=IDB[:], in_=IDB[:], pattern=[[-1, 128]], compare_op=ALU.not_equal,
                                              fill=1.0, base=0, channel_multiplier=1), reads=[rID], writes=[rID])
    act(SCT[:, :, :], SCT[:, :, :], AF.Silu, [rSCT], [rSCT])
    lg = SPT[:, SP_LB:SP_LB + 64]
    lg3 = lg.rearrange("p (l r) -> p l r", l=4)
    mx = SML[:, 0:16]
    vtt(mx, lg3[:, 0, :], lg3[:, 1, :], ALU.max, [rSP], [rLB])
    vtt(mx, mx, lg3[:, 2, :], ALU.max, [rSP, rLB], [rLB])
    vtt(mx, mx, lg3[:, 3, :], ALU.max, [rSP, rLB], [rLB])
    ex = LBT[:, 2, :]
    ex3 = ex.rearrange("p (l r) -> p l r", l=4)
    for l in range(4):
        vtt(ex3[:, l, :], lg3[:, l, :], mx, ALU.subtract, [rSP, rLB], [rLB])
    act(ex, ex, AF.Exp, [rLB], [rLB])
    sm = SML[:, 16:32]
    vtt(sm, ex3[:, 0, :], ex3[:, 1, :], ALU.add, [rLB], [rLB])
    vtt(sm, sm, ex3[:, 2, :], ALU.add, [rLB], [rLB])
    vtt(sm, sm, ex3[:, 3, :], ALU.add, [rLB], [rLB])
    rec.op('dve', lambda e: e.reciprocal(out=sm, in_=sm), reads=[rLB], writes=[rLB])
    for l in range(4):
        vtt(ex3[:, l, :], ex3[:, l, :], sm, ALU.mult, [rLB], [rLB])
    lb3 = LBT[:, 0, :].rearrange("p (l r) -> p l r", l=4)
    rec.op('dve', lambda e: e.memset(lb3[:, 0, :], 0.0), reads=[rLB], writes=[rLB])
    for l in range(1, 4):
        vtt(lb3[:, l, :], lb3[:, l - 1, :], ex3[:, l - 1, :], ALU.add, [rLB], [rLB])
    vts(LBT[:, 1, :], LBT[:, 0, :], -1.0, 1.0, ALU.mult, ALU.add, [rLB], [rLB])
    rec.barrier()

    cc_id = [0]
    ex_sets = {}
    ex_cnt = {}
    last_ex = [None]

    def exchange(src_sb, ncols, dtype, src_res, tag):
        k = cc_id[0]
        cc_id[0] += 1
        key = 'f' if dtype == F32 else 'b'
        full = 256 if dtype == F32 else 512
        if key not in ex_sets:
            ex_sets[key] = []
            for i in range(4):
                ex_sets[key].append((nc.dram_tensor("ein%s%d" % (key, i), [128, full], dtype),
                                     nc.dram_tensor("eout%s%d" % (key, i), [256, full], dtype), Res('ein'), Res('eout')))
            ex_cnt[key] = 0
        ein_t, eout_t, rin, rout = ex_sets[key][ex_cnt[key] % 4]
        ex_cnt[key] += 1
        dma('sp', ein_t.ap()[:, 0:ncols], src_sb, [src_res], [rin])
        if k >= CC_LIMIT:
            return eout_t.ap(), rout
        rec.op('pool', lambda e: e.collective_compute("AllGather", ALU.bypass, replica_groups=[[0, 1], [2, 3], [4, 5], [6, 7]],
                                                       ins=[ein_t.ap().opt()], outs=[eout_t.ap().opt()]),
               reads=[rin], writes=[rout], kind='cc')
        last_ex[0] = (eout_t.ap(), rout)
        return eout_t.ap(), rout

    mA = spc(SP_PM + 0)
    mB = spc(SP_PM + 1)

    def _layers():
      for l in range(nl):
          if STOP <= 0:
              raise _StopBuild()
          rAda = [Res('ada0'), Res('ada1')]
          PM = P7[:, 0:192]
          for ch in range(N_ADA):
              slot = ch % 2
              wa = fv(O_T + slot * 2048, 2048)
              dma('pool', wa, wada[l, ch], [], [rAda[slot]])
              wa3 = wa.rearrange("p (a b) -> p a b", a=KC)
              for kc in range(KC):
                  mm(PM[:, ch * 2:ch * 2 + 2], wa3[:, kc, :], SCT[:, kc, :], kc == 0, kc == KC - 1, [rAda[slot], rSCT], [rP7a])
          rPL = Res('pl')
          dma('sp', PLT[:, :], plin[l], [], [rPL])
          bada = PLT[:, PL_BADA:PL_BADA + 96]
          for w in range(2):
              vtt(MODT[:, :, w], P7[:, 0:192].rearrange("p (a b) -> p a b", b=2)[:, :, w], bada, ALU.add, [rP7a, rPL], [rMOD])
          vts(SC1P[:, 0:16, :], MODT[:, 16:32, :], 1.0, None, ALU.add, None, [rMOD], [rMOD])
          vts(SC1P[:, 16:32, :], MODT[:, 64:80, :], 1.0, None, ALU.add, None, [rMOD], [rMOD])
          sh1 = MODT[:, 0:16, :]
          g1 = MODT[:, 32:48, :]
          sh2 = MODT[:, 48:64, :]
          g2 = MODT[:, 80:96, :]
          rec.barrier()
          if STOP <= 1:
              raise _StopBuild()

          for kc in range(KC):
              for (c0, c1, w) in CR:
                  act(hT[:, kc, c0:c1], xT[:, kc, c0:c1], AF.Identity, [rX, rMOD], [rH],
                      bias=sh1[:, kc, w:w + 1], scale=SC1P[:, kc, w:w + 1])
          rXS = Res('xs')
          for kc in range(KC):
              dma('pool', xs[:, kc * NT:(kc + 1) * NT], xT[:, kc, :], [rX], [rXS])
          rec.barrier()

          if STOP <= 2:
              raise _StopBuild()
          ws = WStream(l)
          o = [O_X + 15360]

          def alloc(n, pool=o):
              off = pool[0]
              pool[0] += n
              return off
          qs = fv(alloc(1280), 1280)
          oacc = fv(alloc(2560), 2560)
          tF = fv(alloc(1280), 1280)
          assert o[0] <= O_X + 20480
          pu = [O_U]
          V2 = bv(alloc(1280, pu), 1280)
          sgA = bv(alloc(640, pu), 640)
          sgB = bv(alloc(640, pu), 640)
          QT = bv(alloc(640, pu), 640)
          QTB = bv(alloc(640, pu), 640)
          KT = bv(alloc(640, pu), 640)
          KH = bv(alloc(640, pu), 640)
          ks = fv(alloc(1280, pu), 1280)
          tA = fv(alloc(1280, pu), 1280)
          rxoff = pu[0] - 1280
          tB = fv(alloc(1280, pu), 1280)
          Sst = fv(alloc(256, pu), 256)
          Sb = bv(alloc(128, pu), 128)
          rmk = bv(alloc(640, pu), 640)
          PTb = bv(alloc(128, pu), 128)
          sclr = fv(alloc(64, pu), 64)
          assert pu[0] <= O_U + 10240, pu[0]
          pt = [O_T]
          tC = fv(alloc(1280, pt), 1280)
          tD = fv(alloc(1280, pt), 1280)
          tE = fv(alloc(1280, pt), 1280)
          KHT = AR[:, pt[0] - 1280:pt[0] - 640].bitcast(BF16)
          cosT = fv(alloc(1280, pt), 1280)
          sinT = fv(alloc(1280, pt), 1280)
          assert pt[0] <= NW, pt[0]
          rx = fv(rxoff, 512)
          qouts = [None]
          r = {k: Res(k) for k in ('qs', 'oacc', 'tF', 'V2', 'sgA', 'sgB', 'QT', 'KT', 'KHT', 'KH', 'ks', 'tA', 'tB',
                                   'S', 'Sb', 'PT', 'sclr', 'tC', 'tD', 'tE', 'rmk', 'cos', 'sin')}
          r['KHT'] = r['tE']
          r['rx'] = r['tA']

          def chunk_prep(ps, pres, la_ready, C, d, qsrc, qres, qscale, ksrc, kres):
              nchk = NT // C
              rec.op('dve', lambda e: e.tensor_tensor_scan(out=tD, data0=rmk, data1=tC, initial=0.0, op0=ALU.mult, op1=ALU.add),
                     reads=[r['rmk'], r['tC']], writes=[r['tD']])
              b3 = tD.rearrange("p (a b) -> p a b", b=C)
              e3 = tE.rearrange("p (a b) -> p a b", b=C)
              la3 = tC.rearrange("p (a b) -> p a b", b=C)
              if d == 1:
                  tot = b3[:, :, C - 1:C].to_broadcast([128, nchk, C])
                  vtt(e3, tot, b3, ALU.subtract, [r['tD']], [r['tE']])
                  vtt(b3, e3, la3, ALU.add, [r['tE'], r['tC']], [r['tD']])
                  imid, iend = C // 2, 0
              else:
                  imid, iend = C // 2 - 1, C - 1
              bmid = b3[:, :, imid:imid + 1]
              bend = b3[:, :, iend:iend + 1]
              act(sclr[:, 0:nchk], b3[:, :, imid], AF.Exp, [r['tD']], [r['sclr']])
              act(sclr[:, 20:20 + nchk], b3[:, :, iend], AF.Exp, [r['tD']], [r['sclr']])
              vtt(e3, b3, bmid.to_broadcast([128, nchk, C]), ALU.subtract, [r['tD']], [r['tE']])
              vts(tE, tE, 40.0, -40.0, ALU.min, ALU.max, [r['tE']], [r['tE']])
              act(tB, tE, AF.Exp, [r['tE']], [r['tB']])
              for (qap, psl) in qouts[0]:
                  vstt(qap[psl, :], qsrc[psl, :], qscale, tB[psl, :], ALU.mult, ALU.mult, [qres, r['tB']], [r['QT']])
              act(tB, tE, AF.Exp, [r['tE'], r['QT']], [r['tB']], scale=-1.0)
              vtt(KT, ksrc, tB, ALU.mult, [kres, r['tB']], [r['KT']])
              vtt(e3, bend.to_broadcast([128, nchk, C]), b3, ALU.subtract, [r['tD'], r['KT']], [r['tE']])
              act(tB, tE, AF.Exp, [r['tE'], r['KT']], [r['tB']])
              vtt(KHT, ksrc, tB, ALU.mult, [kres, r['tB']], [r['KHT']])
              for (ta, tb) in ((0, 8), (8, 10)):
                  for tt in range(ta, tb):
                      rec.op('pe', lambda e, tt=tt, ta=ta: e.transpose(out=P6[:, (tt - ta) * 128:(tt - ta + 1) * 128],
                                                                        in_=KHT[:, tt * 128:(tt + 1) * 128], identity=IDB[:]),
                             reads=[r['KHT'], rID], writes=[rP6])
                  act(KH[:, ta * 128:tb * 128], P6[:, 0:(tb - ta) * 128], AF.Identity, [rP6], [r['KH']])

          def chunk_loop(C, d, nheads, mask_k, segs, first_zero_segs, evac_add):
              nch = 128 // C
              V3 = V2.rearrange("p (t c) -> p t c", t=10)
              KH3 = KH.rearrange("p (t c) -> p t c", t=10)
              oacc3 = oacc.rearrange("p (h t) -> p h t", h=2)
              SW = 128 * nheads
              for (tiles, zero_init) in segs:
                  state_zero = zero_init
                  for tt in tiles:
                      cis = list(range(nch)) if d == 0 else list(range(nch - 1, -1, -1))
                      for hh in range(nheads):
                          QTh = qouts[0][hh][0]
                          mm(P7[:, hh * 128:(hh + 1) * 128], KT[:, tt * 128:(tt + 1) * 128], QTh[:, tt * 128:(tt + 1) * 128],
                             True, True, [r['KT'], r['QT']], [rP7a])
                      for hh in range(nheads):
                          vtt(PTb[:, hh * 128:(hh + 1) * 128], P7[:, hh * 128:(hh + 1) * 128], maskT(mask_k), ALU.mult,
                              [rP7a, rCF], [r['PT']])
                      any_inter = not (state_zero and nch == 1)
                      def oreg(hh):
                          return P7[:, 256:384] if hh == 0 else PB[:, 512:640]
                      for hh in range(nheads):
                          ores = rP7b if hh == 0 else rPB
                          mm(oreg(hh), V3[:, tt, hh * 128:(hh + 1) * 128],
                             PTb[:, hh * 128:(hh + 1) * 128], True, state_zero and nch == 1, [r['V2'], r['PT']], [ores])
                          if nheads == 2 and not state_zero:
                              QTh = qouts[0][hh][0]
                              mm(oreg(hh), Sb[:, hh * 128:(hh + 1) * 128], QTh[:, tt * 128:(tt + 1) * 128], False, True,
                                 [r['Sb'], r['QT']], [ores])
                      for n_ci, ci in enumerate(cis):
                          c0 = tt * 128 + ci * C
                          chunk = tt * nch + ci
                          if not state_zero and nheads == 1:
                              for hh in range(nheads):
                                  QTh = qouts[0][hh][0]
                                  mm(P7[:, 256 + hh * 128 + ci * C:256 + hh * 128 + ci * C + C], Sb[:, hh * 128:(hh + 1) * 128],
                                     QTh[:, c0:c0 + C], False, True, [r['Sb'], r['QT']], [rP7b])
                          rows = slice(ci * C, ci * C + C)
                          mm(PB[:, 0:SW], KH3[rows, tt, :], V3[rows, tt, 0:SW], True, True, [r['KH'], r['V2']], [rPB])
                          if state_zero:
                              rec.op('dve', lambda e: e.tensor_copy(out=Sst[:, 0:SW], in_=PB[:, 0:SW]), reads=[rPB], writes=[r['S']])
                          else:
                              vstt(Sst[:, 0:SW], Sst[:, 0:SW], sclr[:, 20 + chunk:21 + chunk], PB[:, 0:SW], ALU.mult, ALU.add,
                                   [r['S'], rPB, r['sclr']], [r['S']])
                          state_zero = False
                          nxt = None
                          if n_ci + 1 < len(cis):
                              nxt = tt * nch + cis[n_ci + 1]
                          else:
                              ti = tiles.index(tt)
                              if ti + 1 < len(tiles):
                                  nxt = tiles[ti + 1] * nch + (0 if d == 0 else nch - 1)
                          if nxt is not None:
                              act(Sb[:, 0:SW], Sst[:, 0:SW], AF.Identity, [r['S'], r['sclr']], [r['Sb']], scale=sclr[:, nxt:nxt + 1])
                      for hh in range(nheads):
                          dst = oacc3[:, hh, tt * 128:(tt + 1) * 128]
                          src = oreg(hh)
                          ores = rP7b if hh == 0 else rPB
                          if evac_add:
                              vtt(dst, dst, src, ALU.add, [ores, r['oacc']], [r['oacc']])
                          else:
                              act(dst, src, AF.Identity, [ores], [r['oacc']])

          def finalize(hh, gcol, sg, sgres, slot):
              o_ = oacc[:, hh * 1280:(hh + 1) * 1280]
              act(tA, o_, AF.Square, [r['oacc']], [r['tA']])
              for (t0, t1) in TT:
                  mm(PA[:, t0:t1], ones, tA[:, t0:t1], True, True, [r['tA'], rCF], [rPA])
              vts(tB, PA[:, 0:NT], 1.0 / 128.0, 1e-6, ALU.mult, ALU.add, [rPA], [r['tB']])
              act(tB, tB, AF.Sqrt, [r['tB']], [r['tB']])
              rec.op('dve', lambda e: e.reciprocal(out=tB, in_=tB), reads=[r['tB']], writes=[r['tB']])
              vstt(tA, o_, gcol, tB, ALU.mult, ALU.mult, [r['oacc'], r['tB'], rSP], [r['tA']])
              vtt(yT[:, slot, :], tA, sg, ALU.mult, [r['tA'], sgres], [rYs[slot]])

          def recv_state(eo, ro, SW):
              dma('sp', rx.rearrange("p (h c) -> p h c", h=2)[:, :, 0:SW], eo.rearrange("(h p) c -> p h c", h=2)[:, :, 0:SW], [ro], [r['rx']])
              rx3 = rx.rearrange("p (h c) -> p h c", h=2)
              vts(Sst[:, 0:SW], rx3[:, 0, 0:SW], mA, None, ALU.mult, None, [r['rx'], rSP], [r['S']])
              vstt(Sst[:, 0:SW], rx3[:, 1, 0:SW], mB, Sst[:, 0:SW], ALU.mult, ALU.add, [r['rx'], rSP, r['S']], [r['S']])

          dma('pool', rmk, rmin[:, 0, :], [], [r['rmk']])
          qouts[0] = [(QT, slice(0, 128))]
          for j in range(8):
              if j >= NH_A:
                  ws.skip(5)
                  continue
              wq, rq_ = ws.next()
              proj_fm(w3of(wq), rq_, hT, rH, PA, rPA)
              act(qs, PA[:, 0:NT], AF.Silu, [rPA], [r['qs']])
              wi, ri_ = ws.next()
              proj_tm(w3of(wi), ri_, hT, rH, PB, rPB)
              V3w = V2.rearrange("p (t c) -> p t c", t=10)
              act(V3w[:, :, 0:128], PB[:, 0:NT].rearrange("p (t c) -> p t c", t=10), AF.Identity, [rPB], [r['V2']])
              wg, rg_ = ws.next()
              proj_fm(w3of(wg), rg_, hT, rH, PA, rPA)
              act(sgA, PA[:, 0:NT], AF.Silu, [rPA], [r['sgA']])
              for d in range(2):
                  wf, rf_ = ws.next()
                  proj_fm(w3of(wf), rf_, hT, rH, PA, rPA)
                  lbc = LBT[:, 0, l * 16 + d * 8 + j:l * 16 + d * 8 + j + 1]
                  omc = LBT[:, 1, l * 16 + d * 8 + j:l * 16 + d * 8 + j + 1]
                  act(tA, PA[:, 0:NT], AF.Sigmoid, [rPA], [r['tA']])
                  vts(tB, tA, omc, lbc, ALU.mult, ALU.add, [r['tA'], rLB], [r['tB']])
                  act(tC, tB, AF.Ln, [r['tB']], [r['tC']])
                  vts(ks, tB, -1.0, 1.0, ALU.mult, ALU.add, [r['tB']], [r['ks']])
                  chunk_prep(None, None, None, 64, d, qs, r['qs'], 128.0 ** -0.5, ks, r['ks'])
                  if d == 0:
                      chunk_loop(64, 0, 1, 0, [(list(range(10)), True)], None, False)
                      eo, ro = exchange(Sst[:, 0:128], 128, F32, r['S'], 'a%d' % j)
                  else:
                      recv_state(eo, ro, 128)
                      act(Sb[:, 0:128], Sst[:, 0:128], AF.Identity, [r['S'], r['sclr']], [r['Sb']], scale=sclr[:, 19:20])
                      chunk_loop(64, 1, 1, 1, [(list(range(9, 1, -1)), False), ([1, 0], True)], None, True)
              finalize(0, spc(SP_HNG + l), sgA, r['sgA'], j)
              if STOP <= 3:
                  rec.barrier()
                  act(xT[:, 0, :], yT[:, 0, :], AF.Identity, [], [rX])
                  act(xT[:, 1, :], oacc[:, 0:1280], AF.Identity, [], [rX])
                  act(xT[:, 2, :], qs, AF.Identity, [], [rX])
                  act(xT[:, 3, 0:128], Sst[:, 0:128], AF.Identity, [], [rX])
                  raise _StopBuild()

          if STOP <= 3.1:
              raise _StopBuild()
          dma('pool', rmk, rmin[:, 1, :], [], [r['rmk']])
          qouts[0] = [(QT, slice(0, 64)), (QTB, slice(64, 128))]
          rec.op('dve', lambda e: e.memset(QT[64:128, :], 0.0), reads=[], writes=[r['QT']])
          rec.op('dve', lambda e: e.memset(QTB[0:64, :], 0.0), reads=[], writes=[r['QT']])
          dma('pool', WGK[:, :], wgkin[:, l, :], [], [rWGK])
          dma('sp', cosT, ropein[:, 0, :], [], [r['cos']])
          dma('sp', sinT, ropein[:, 1, :], [], [r['sin']])
          wk_, rk_ = ws.next()
          proj_fm(w3of(wk_), rk_, hT, rH, PA, rPA)
          lo = AR[:, O_X + 15360 + 3840:O_X + 15360 + 3840 + 640].bitcast(BF16)
          act(lo[0:64, :], PA[0:64, 0:NT], AF.Identity, [rPA], [r['tF']])
          if STOP <= 3.2:
              raise _StopBuild()
          for p in range(4):
              for which in range(2):
                  w1_, r1_ = ws.next()
                  proj_fm(w3of(w1_), r1_, hT, rH, PA, rPA)
                  w2_, r2_ = ws.next()
                  proj_fm(w3of(w2_), r2_, hT, rH, PB, rPB)
                  dst, dres = (qs, r['qs']) if which == 0 else (ks, r['ks'])
                  vtt(tA, PA[:, 0:NT], cosT, ALU.mult, [rPA, r['cos']], [r['tA']])
                  vtt(tB, PB[:, 0:NT], sinT, ALU.mult, [rPB, r['sin']], [r['tB']])
                  vtt(dst, tA, tB, ALU.add, [r['tA'], r['tB']], [dres])
              V3w = V2.rearrange("p (t c) -> p t c", t=10)
              for hh in range(2):
                  wv, rv_ = ws.next()
                  proj_tm(w3of(wv), rv_, hT, rH, PB, rPB)
                  act(V3w[:, :, hh * 128:(hh + 1) * 128], PB[:, 0:NT].rearrange("p (t c) -> p t c", t=10), AF.Identity, [rPB], [r['V2']])
              for hh, (sg, sres) in enumerate(((sgA, r['sgA']), (sgB, r['sgB']))):
                  wg, rg_ = ws.next()
                  proj_fm(w3of(wg), rg_, hT, rH, PA, rPA)
                  act(sg, PA[:, 0:NT], AF.Silu, [rPA], [sres])
              if STOP <= 3.3:
                  raise _StopBuild()
              for d in range(2):
                  for (t0, t1) in TT:
                      mm(PA[:, t0:t1], WGK[32 * d:32 * d + 32, p * 128:(p + 1) * 128], lo[32 * d:32 * d + 32, t0:t1], True, True,
                         [rWGK, r['tF']], [rPA])
                  act(tA, PA[:, 0:NT], AF.Sigmoid, [rPA, rSP], [r['tA']], bias=spc(SP_BGK + l * 8 + d * 4 + p))
                  act(tB, tA, AF.Ln, [r['tA']], [r['tB']])
                  vts(tC, tB, 1.0 / 16.0, None, ALU.mult, None, [r['tB']], [r['tC']])
                  chunk_prep(None, None, None, 128, d, qs, r['qs'], 64.0 ** -0.5, ks, r['ks'])
                  if STOP <= 3.4:
                      raise _StopBuild()
                  if d == 0:
                      chunk_loop(128, 0, 2, 2, [(list(range(10)), True)], None, False)
                      if STOP <= 3.5:
                          raise _StopBuild()
                      eo, ro = exchange(Sst[:, 0:256], 256, F32, r['S'], 'b%d' % p)
                  else:
                      recv_state(eo, ro, 256)
                      if STOP <= 3.6:
                          raise _StopBuild()
                      act(Sb[:, 0:256], Sst[:, 0:256], AF.Identity, [r['S'], r['sclr']], [r['Sb']], scale=sclr[:, 9:10])
                      chunk_loop(128, 1, 2, 3, [(list(range(9, 1, -1)), False), ([1, 0], True)], None, True)
              finalize(0, spc(SP_GNG + l), sgA, r['sgA'], 8 + 2 * p)
              finalize(1, spc(SP_GNG + l), sgB, r['sgB'], 8 + 2 * p + 1)
              if STOP <= 3.7:
                  for _i in range(N_DUMMY):
                      mm(PA[:, 0:128], ones, tA[:, 0:128], True, True, [r['tA'], rCF], [rPA])
                  for _i in range(N_DUMMY3):
                      vts(tA[:, 0:128], tA[:, 0:128], 1.0, None, ALU.mult, None, [r['tA']], [r['tA']])
                      act(tB[:, 0:128], tB[:, 0:128], AF.Identity, [r['tB']], [r['tB']])
                  for _i in range(N_DUMMY2):
                      dma('pool', bv(O_WB, 1024), wsc[l, _i % 8], [], [r['tC']])
                  raise _StopBuild()
          rec.barrier()
          if STOP <= 4:
              raise _StopBuild()

          pu2 = [O_U]
          qTn = bv(alloc(640, pu2), 640)
          kE = bv(alloc(768, pu2), 768)
          vE = bv(alloc(768, pu2), 768)
          exb = bv(alloc(256, pu2), 256)
          rxn = bv(alloc(512, pu2), 512)
          rxv = bv(alloc(256, pu2), 256)
          tS = fv(alloc(896, pu2), 896)
          pB = bv(alloc(448, pu2), 448)
          pTs = bv(alloc(448, pu2), 448)
          onb = bv(alloc(64, pu2), 64)
          nbt = [fv(alloc(640, pu2), 640), fv(alloc(640, pu2), 640)]
          assert pu2[0] <= O_U + 10240, pu2[0]
          rn = {k: Res(k) for k in ('q', 'k', 'v', 'exb', 'rxn', 'rxv', 'tS', 'pB', 'pTs', 'on', 'nb0', 'nb1', 'st')}
          vE3 = vE.rearrange("p (t c) -> p t c", t=12)
          stat = SML[:, 32:40]
          nbk = 0
          for j in range(8):
              wq, rq_ = ws.next()
              proj_fm(w3of(wq), rq_, hT, rH, PA, rPA)
              act(qTn, PA[:, 0:NT], AF.Identity, [rPA], [rn['q']], scale=128.0 ** -0.5)
              wk2, rk2 = ws.next()
              proj_fm(w3of(wk2), rk2, hT, rH, PB, rPB)
              act(kE[:, 0:NT], PB[:, 0:NT], AF.Identity, [rPB], [rn['k']])
              wv, rv_ = ws.next()
              proj_tm(w3of(wv), rv_, hT, rH, PA, rPA)
              act(vE3[:, 0:10, :], PA[:, 0:NT].rearrange("p (t c) -> p t c", t=10), AF.Identity, [rPA], [rn['v']])
              rec.op('dve', lambda e: e.tensor_copy(out=exb[:, 0:256], in_=kE[:, 1024:1280]), reads=[rn['k']], writes=[rn['exb']])
              rec.op('dve', lambda e: e.tensor_copy(out=exb[:, 256:512].rearrange("p (t c) -> p t c", t=2), in_=vE3[:, 8:10, :]),
                     reads=[rn['v']], writes=[rn['exb']])
              eo, ro = exchange(exb, 512, BF16, rn['exb'], 'n%d' % j)
              eo3 = eo.rearrange("(h p) c -> p h c", h=2)
              rxn3 = rxn.rearrange("p (h c) -> p h c", h=2)
              dma('sp', rxn3[:, :, 0:256], eo3[:, :, 0:256], [ro], [rn['rxn']])
              rxv4 = rxv.rearrange("p (h t c) -> p h t c", h=2, t=2)
              eo4 = eo.rearrange("(h p) (x t c) -> p h x t c", h=2, x=2, t=2)
              for hm in range(2):
                  for hp in range(2):
                      dma('sp', rxv4[hp * 64:(hp + 1) * 64, :, hm, :], eo4[(1 - hp) * 64:(2 - hp) * 64, :, 1, 1 - hm, :], [ro], [rn['rxv']])
              for m in range(4):
                  dstk = kE[:, 1280 + m * 64:1280 + (m + 1) * 64]
                  s0 = rxn3[:, 0, (3 - m) * 64:(4 - m) * 64]
                  s1 = rxn3[:, 1, (3 - m) * 64:(4 - m) * 64]
                  vts(dstk, s0, mA, None, ALU.mult, None, [rn['rxn'], rSP], [rn['k']])
                  vstt(dstk, s1, mB, dstk, ALU.mult, ALU.add, [rn['rxn'], rSP, rn['k']], [rn['k']])
              for hm in range(2):
                  dstv = vE3[:, 10 + hm, :]
                  vts(dstv, rxv4[:, 0, hm, :], mA, None, ALU.mult, None, [rn['rxv'], rSP], [rn['v']])
                  vstt(dstv, rxv4[:, 1, hm, :], mB, dstv, ALU.mult, ALU.add, [rn['rxv'], rSP, rn['v']], [rn['v']])
              if STOP <= 4.1:
                  raise _StopBuild()
              for qt in range(10):
                  if qt < 8:
                      qc0 = 256 + 128 * qt
                      lo_ = min(max(2 * qt - 4, 0), 10)
                      w0 = 256 + 64 * lo_
                      nk_ = 896
                      nb = nbt[nbk % 2]
                      nbr = rn['nb%d' % (nbk % 2)]
                      nbk += 1
                      dma('pool', nb, nbin[l, j, qt], [], [nbr])
                      mm(PA[:, 0:512], qTn[:, qc0:qc0 + 128], kE[:, w0:w0 + 512], True, True, [rn['q'], rn['k']], [rPA])
                      mm(PA[:, 512:640], qTn[:, qc0:qc0 + 128], kE[:, w0 + 512:w0 + 640], True, True, [rn['q'], rn['k']], [rPA])
                      mm(PA[:, 640:896], qTn[:, qc0:qc0 + 128], kE[:, 0:256], True, True, [rn['q'], rn['k']], [rPA])
                      vtt(tS[:, 0:640], PA[:, 0:640], nb, ALU.add, [rPA, nbr], [rn['tS']])
                      act(tS[:, 640:896], PA[:, 640:896], AF.Identity, [rPA], [rn['tS']])
                      vblocks = [2 + lo_ // 2 + b for b in range(5)] + [0, 1]
                  else:
                      qc0 = 128 * (qt - 8)
                      nk_ = 256
                      mm(PA[:, 0:256], qTn[:, qc0:qc0 + 128], kE[:, 0:256], True, True, [rn['q'], rn['k']], [rPA])
                      act(tS[:, 0:256], PA[:, 0:256], AF.Identity, [rPA], [rn['tS']])
                      vblocks = [0, 1]
                  rec.op('dve', lambda e, nk_=nk_: e.reduce_max(out=stat[:, 0:1], in_=tS[:, 0:nk_], axis=mybir.AxisListType.X),
                         reads=[rn['tS']], writes=[rn['st']])
                  vts(stat[:, 1:2], stat[:, 0:1], -1.0, None, ALU.mult, None, [rn['st']], [rn['st']])
                  rec.op('dve', lambda e: e.memset(stat[:, 2:3], 0.0), reads=[rn['st']], writes=[rn['st']])
                  act(pB[:, 0:nk_], tS[:, 0:nk_], AF.Exp, [rn['tS'], rn['st']], [rn['pB'], rn['st']], bias=stat[:, 1:2], accum=stat[:, 2:3])
                  rec.op('dve', lambda e: e.reciprocal(out=stat[:, 3:4], in_=stat[:, 2:3]), reads=[rn['st']], writes=[rn['st']])
                  nb_ = nk_ // 128
                  for b in range(nb_):
                      rec.op('pe', lambda e, b=b: e.transpose(out=P6[:, b * 128:(b + 1) * 128], in_=pB[:, b * 128:(b + 1) * 128], identity=IDB[:]),
                             reads=[rn['pB'], rID], writes=[rP6])
                  act(pTs[:, 0:nk_], P6[:, 0:nk_], AF.Identity, [rP6], [rn['pTs']])
                  for b in range(nb_):
                      mm(P7[:, 0:128], pTs[:, b * 128:(b + 1) * 128], vE3[:, vblocks[b], :], b == 0, b == nb_ - 1, [rn['pTs'], rn['v']], [rP7a])
                  act(onb, P7[:, 0:128], AF.Identity, [rP7a, rn['st']], [rn['on']], scale=stat[:, 3:4])
                  rec.op('pe', lambda e: e.transpose(out=P6[:, 896:1024], in_=onb, identity=IDB[:]), reads=[rn['on'], rID], writes=[rP6])
                  act(yT[:, 16 + j, qc0:qc0 + 128], P6[:, 896:1024], AF.Identity, [rP6], [rYs[16 + j]])
              if STOP <= 4.2:
                  raise _StopBuild()
          rec.barrier()

          if DEBUG and l == 0:
              dma('pool', dbgy[:, :, :], yT, [], [])
              rec.barrier()
          if STOP <= 5:
              raise _StopBuild()
          pt2 = [O_T]
          uacc = [fv(alloc(1280, pt2), 1280), fv(alloc(1280, pt2), 1280)]
          tG = fv(alloc(1280, pt2), 1280)
          tP = fv(alloc(1280, pt2), 1280)
          rg = {k: Res(k) for k in ('ua0', 'ua1', 'tG', 'tP')}
          for op_ in range(8):
              for n in range(3):
                  wbr, rbr = ws.next()
                  wbr3 = wbr.rearrange("p (a b) -> p a b", a=8)
                  for oo in range(2):
                      for (t0, t1) in TT:
                          for kc in range(8):
                              mm(PA[:, t0:t1], wbr3[:, kc, oo * 128:(oo + 1) * 128], yT[:, n * 8 + kc, t0:t1], kc == 0, kc == 7,
                                 [rbr, rYs[n * 8 + kc]], [rPA])
                      wg_, rwg = ws.next(held=1)
                      proj_fm(w3of(wg_), rwg, hT, rH, PB, rPB)
                      act(tG, PB[:, 0:NT], AF.Sigmoid, [rPB], [rg['tG']])
                      ua, rua = uacc[oo], rg['ua%d' % oo]
                      if n == 0:
                          vtt(ua, PA[:, 0:NT], tG, ALU.mult, [rPA, rg['tG']], [rua])
                      else:
                          vtt(tP, PA[:, 0:NT], tG, ALU.mult, [rPA, rg['tG']], [rg['tP']])
                          vtt(ua, ua, tP, ALU.add, [rua, rg['tP']], [rua])
              for oo in range(2):
                  act(uT[:, 2 * op_ + oo, :], uacc[oo], AF.Identity, [rg['ua%d' % oo]], [rU])
          rec.barrier()

          if STOP <= 6:
              raise _StopBuild()
          for kc in range(KC):
              dma('pool', xT[:, kc, :], xs[:, kc * NT:(kc + 1) * NT], [rXS], [rX])

          def layer_norm(gk, bk):
              tM = fv(O_T, 1280)
              tR = fv(O_T + 1280, 1280)
              tQ = [fv(O_T + 2560, 1280), fv(O_T + 3840, 1280)]
              rl = {k: Res(k) for k in ('tM', 'tR', 'q0', 'q1')}
              for (t0, t1) in TT:
                  for kc in range(KC):
                      mm(PA[:, t0:t1], ones, xT[:, kc, t0:t1], kc == 0, kc == KC - 1, [rXk[kc], rX, rCF], [rPA])
              act(tM, PA[:, 0:NT], AF.Identity, [rPA], [rl['tM']], scale=1.0 / D)
              for kc in range(KC):
                  vtt(xT[:, kc, :], xT[:, kc, :], tM, ALU.subtract, [rl['tM'], rXk[kc], rX], [rXk[kc]])
              for kc in range(KC):
                  q_, rq2 = tQ[kc % 2], rl['q%d' % (kc % 2)]
                  act(q_, xT[:, kc, :], AF.Square, [rXk[kc]], [rq2])
                  for (t0, t1) in TT:
                      mm(PB[:, t0:t1], ones, q_[:, t0:t1], kc == 0, kc == KC - 1, [rq2, rCF], [rPB])
              vts(tR, PB[:, 0:NT], 1.0 / D, 1e-5, ALU.mult, ALU.add, [rPB], [rl['tR']])
              act(tR, tR, AF.Sqrt, [rl['tR']], [rl['tR']])
              rec.op('dve', lambda e: e.reciprocal(out=tR, in_=tR), reads=[rl['tR']], writes=[rl['tR']])
              for kc in range(KC):
                  vtt(xT[:, kc, :], xT[:, kc, :], tR, ALU.mult, [rl['tR'], rXk[kc]], [rXk[kc]])
                  act(xT[:, kc, :], xT[:, kc, :], AF.Identity, [rXk[kc], rPL], [rXk[kc]],
                      bias=PLT[:, PL_LN + bk * 16 + kc:PL_LN + bk * 16 + kc + 1], scale=PLT[:, PL_LN + gk * 16 + kc:PL_LN + gk * 16 + kc + 1])

          for o2 in range(16):
              wo_, rwo = ws.next()
              pp, rpp = (PA, rPA) if o2 % 2 == 0 else (PB, rPB)
              proj_fm(w3of(wo_), rwo, uT, rU, pp, rpp)
              act(xT[:, o2, :], xT[:, o2, :], AF.Identity, [rX, rXk[o2]], [rXk[o2]], scale=ALPHA)
              for (c0, c1, w) in CR:
                  vstt(xT[:, o2, c0:c1], pp[:, c0:c1], g1[:, o2, w:w + 1], xT[:, o2, c0:c1], ALU.mult, ALU.add,
                       [rpp, rMOD, rXk[o2]], [rXk[o2]])
          rec.barrier()
          if STOP <= 6.5:
              raise _StopBuild()
          layer_norm(0, 1)
          rec.barrier()
          if STOP <= 7:
              raise _StopBuild()

          for kc in range(KC):
              for (c0, c1, w) in CR:
                  act(hT[:, kc, c0:c1], xT[:, kc, c0:c1], AF.Identity, [rXk[kc], rMOD], [rH],
                      bias=sh2[:, kc, w:w + 1], scale=SC1P[:, 16 + kc, w:w + 1])
              act(xT[:, kc, :], xT[:, kc, :], AF.Identity, [rXk[kc], rH], [rXk[kc]], scale=ALPHA)
          hid = bv(O_U, 2560).rearrange("p (a b) -> p a b", a=4)
          tRl = [fv(O_T, 1280), fv(O_T + 1280, 1280)]
          rhid = [Res('hid%d' % i) for i in range(4)]
          rrl = [Res('rl0'), Res('rl1')]
          cnt = 0
          for gi in range(16):
              for j in range(4):
                  w1_, r1_ = ws.next()
                  pp, rpp = (PA, rPA) if cnt % 2 == 0 else (PB, rPB)
                  t_, rt_ = tRl[cnt % 2], rrl[cnt % 2]
                  cnt += 1
                  proj_fm(w3of(w1_), r1_, hT, rH, pp, rpp)
                  act(t_, pp[:, 0:NT], AF.Relu, [rpp], [rt_])
                  vtt(hid[:, j, :], t_, t_, ALU.mult, [rt_], [rhid[j]])
              for q in range(4):
                  w2_, r2_ = ws.next()
                  w23 = w2_.rearrange("p (a b) -> p a b", a=4)
                  for oo in range(4):
                      o2 = 4 * q + oo
                      pp, rpp = (PA, rPA) if cnt % 2 == 0 else (PB, rPB)
                      cnt += 1
                      for (t0, t1) in TT:
                          for j in range(4):
                              mm(pp[:, t0:t1], w23[:, j, oo * 128:(oo + 1) * 128], hid[:, j, t0:t1], j == 0, j == 3, [r2_, rhid[j]], [rpp])
                      for (c0, c1, w) in CR:
                          vstt(xT[:, o2, c0:c1], pp[:, c0:c1], g2[:, o2, w:w + 1], xT[:, o2, c0:c1], ALU.mult, ALU.add,
                               [rpp, rMOD, rXk[o2]], [rXk[o2]])
          rec.barrier()
          layer_norm(2, 3)
          rec.barrier()
          for kc in range(KC):
              rXk[kc].lw = None
              rXk[kc].rd = {}
          rX.lw = None
          rX.rd = {}

    try:
        _layers()
    except _StopBuild:
        rec.barrier()

    ods = [dma('pool', outT[:, kc, :], xT[:, kc, NCTX:NT], [], []) for kc in range(KC)]
    rec.prog['sp'].append((ods, None, None, 0, 'w'))

    sems = {}
    for nm in rec.all_sem_names():
        sems[nm] = es.enter_context(nc.semaphore(nm))
    with nc.Block() as block:
        @block.tensor
        def _(e):
            rec.emit('pe', e, sems)

        @block.scalar
        def _(e):
            rec.emit('act', e, sems)

        @block.vector
        def _(e):
            rec.emit('dve', e, sems)

        @block.gpsimd
        def _(e):
            rec.emit('pool', e, sems)

        @block.sync
        def _(e):
            rec.emit('sp', e, sems)
    es.close()
    return nc


def _unit_kinds():
    kinds = []
    for j in range(8):
        kinds += ['c', 'c', 'c', 'd', 'd']
    kinds += ['d']
    for p in range(4):
        kinds += ['c'] * 8
    kinds += ['c'] * 24
    kinds += ['c'] * (8 * 3 * 3)
    kinds += ['c'] * 16
    kinds += ['c'] * (16 * 8)
    return kinds


_UNIT_KINDS_FULL = _unit_kinds()
_UNIT_KINDS = [k for k in _UNIT_KINDS_FULL if k == 'c']

_CACHE = {}


def kernel(**inp):
    inp = {k: np.asarray(v) for k, v in inp.items()}
    nl = LAYERS
    if nl not in _CACHE:
        _CACHE[nl] = build_program(nl)
    nc = _CACHE[nl]
    wsc = np.empty((nl, len(_UNIT_KINDS), 128, 2048), np.float32)
    wsd = [np.empty((nl, 17, 128, 2048), np.float32) for _ in range(2)]
    wada = np.empty((nl, N_ADA, 128, 2048), np.float32)
    for l in range(nl):
        cu = [u for u in _build_common_units(inp, l) if u is not None]
        assert len(cu) == len(_UNIT_KINDS)
        for i, u in enumerate(cu):
            wsc[l, i] = u
        for side in range(2):
            for i, u in enumerate(_build_dir_units(inp, l, side)):
                wsd[side][l, i] = u
        wa = inp['w_ada'][l]
        wada[l] = wa.reshape(KC, 128, N_ADA, 128).transpose(2, 1, 0, 3).reshape(N_ADA, 128, 2048)
    cf = _const_f32()
    plp = _layer_params(inp, nl)
    rm = _reset_masks()
    side_data = []
    for side in range(2):
        valid, dr, dc = _build_nbias_idx(side)
        nb = np.empty((nl, 8, 8, 128, 640), np.float32)
        for l in range(nl):
            g = inp['natten_rpb'][l][:, dr, dc]
            nb[l] = np.where(valid[None], g, np.float32(-1e30))
        cosT, sinT = _rope_tables(side)
        side_data.append(dict(nb=nb, rope=np.ascontiguousarray(np.stack([cosT, sinT], axis=1)),
                              sp=_small_params(inp, side, nl), wgk=_wgk(inp, side, nl)))
    if STOP <= 50:
        wsc = np.ascontiguousarray(wsc[:, :176])
    in_maps = []
    for core in range(8):
        b, side = core // 2, core % 2
        if side == 0:
            ctx_loc = inp['ctx'][b]
            lat_loc = inp['x'][b, :NLAT]
        else:
            ctx_loc = inp['ctx'][b][::-1]
            lat_loc = inp['x'][b, NLAT:][::-1]
        X = np.concatenate([ctx_loc, lat_loc], axis=0)
        xin = np.ascontiguousarray(X.T.reshape(KC, 128, NT).transpose(1, 0, 2))
        cpair = np.stack([inp['c_ctx'], inp['c'][b]], axis=0)
        cin = np.ascontiguousarray(cpair.T.reshape(KC, 128, 2).transpose(1, 0, 2))
        sd = side_data[side]
        in_maps.append(dict(xin=xin, cin=cin, wada=wada, wsc=wsc, wsd=wsd[side], spin=sd['sp'], plin=plp, cfin=cf, rmin=rm,
                            ropein=sd['rope'], wgkin=sd['wgk'], nbin=sd['nb']))
    res = run_bass_kernel_spmd(nc, in_maps, core_ids=list(range(8)))
    if DEBUG:
        global _DBG
        _DBG = [np.asarray(res.results[c]["dbgy"]) for c in range(8)]
    out = np.empty((4, 2048, D), np.float32)
    for core in range(8):
        b, side = core // 2, core % 2
        oT = np.asarray(res.results[core]["outT"])
        lat = oT.transpose(2, 1, 0).reshape(NLAT, D)
        if side == 0:
            out[b, :NLAT] = lat
        else:
            out[b, NLAT:] = lat[::-1]
    return out
```
